# Optimizing a Trainium2 kernel written in Bass

```python
import math
import jax, jax.numpy as jnp
from jax import lax
import numpy as np

D_MODEL = 1024
BATCH = 8
SEQ = 4096
DEPTH = 4

ATT_HEADS = 8
HEAD_DIM = 64
ATT_WIDTH = ATT_HEADS * HEAD_DIM
CONV_GROUPS = 8
CONV_WIDTH = D_MODEL - ATT_WIDTH
CONV_K = 3
FFN_CONV_K = 3
D_FF = 2816
Q_BLOCK = 128
RMS_EPS = 1e-6
IN_COLS = 3 * ATT_WIDTH + 3 * CONV_WIDTH + 2 * D_MODEL
IN_SPLITS = (ATT_WIDTH, 2 * ATT_WIDTH, 3 * ATT_WIDTH,
             3 * ATT_WIDTH + CONV_WIDTH, 3 * ATT_WIDTH + 2 * CONV_WIDTH,
             3 * ATT_WIDTH + 3 * CONV_WIDTH, 3 * ATT_WIDTH + 3 * CONV_WIDTH + D_MODEL)

kernel_name = "hybrid_stickbreak_shortconv_convffn"


def rms_norm(x, g):
    xf = x.astype(jnp.float32)
    var = jnp.mean(xf * xf, axis=-1, keepdims=True)
    return (xf * lax.rsqrt(var + RMS_EPS) * g.astype(jnp.float32)).astype(x.dtype)


def causal_dwconv(u, w):
    kw = w.shape[0]
    s = u.shape[1]
    up = jnp.pad(u, ((0, 0), (kw - 1, 0), (0, 0)))
    y = up[:, 0:s, :] * w[0]
    for i in range(1, kw):
        y = y + up[:, i:i + s, :] * w[i]
    return y


def stick_breaking_attention(q, k, v):
    _, _, s, dh = q.shape
    scale = 1.0 / math.sqrt(dh)
    outs = []
    for blk in range(s // Q_BLOCK):
        t0 = blk * Q_BLOCK
        t1 = t0 + Q_BLOCK
        qb = q[:, :, t0:t1]
        kb = k[:, :, :t1]
        vb = v[:, :, :t1]
        z = jnp.einsum("bhqd,bhkd->bhqk", qb, kb).astype(jnp.float32) * scale
        t_idx = t0 + jnp.arange(Q_BLOCK)[:, None]
        s_idx = jnp.arange(t1)[None, :]
        strict = s_idx < t_idx
        log_keep = jnp.where(strict, jax.nn.log_sigmoid(-z), 0.0)
        later = lax.cumsum(log_keep, axis=3, reverse=True) - log_keep
        w = jnp.where(strict, jnp.exp(jax.nn.log_sigmoid(z) + later), 0.0)
        outs.append(jnp.einsum("bhqk,bhkd->bhqd", w, vb.astype(jnp.float32)))
    return jnp.concatenate(outs, axis=2).astype(q.dtype)


def setup_inputs(seed: int = 0) -> dict:
    key = jax.random.key(seed)
    ks = jax.random.split(key, 14)
    f32 = jnp.float32

    def nrm(k, shape, fan_in):
        return jax.random.normal(k, shape, f32) * (fan_in ** -0.5)

    def gain(k):
        return 1.0 + 0.05 * jax.random.normal(k, (DEPTH, D_MODEL), f32)

    return {
        "x": jax.random.normal(ks[0], (BATCH, SEQ, D_MODEL), f32),
        "norm_mix_pre": gain(ks[1]),
        "w_in": nrm(ks[2], (DEPTH, D_MODEL, IN_COLS), D_MODEL),
        "conv_mix_w": nrm(ks[3], (DEPTH, CONV_K, CONV_WIDTH), CONV_K),
        "w_att_branch": nrm(ks[4], (DEPTH, ATT_WIDTH, D_MODEL), ATT_WIDTH),
        "w_conv_branch": nrm(ks[5], (DEPTH, CONV_WIDTH, D_MODEL), CONV_WIDTH),
        "w_out": nrm(ks[6], (DEPTH, D_MODEL, D_MODEL), D_MODEL),
        "norm_mix_post": gain(ks[7]),
        "norm_ffn_pre": gain(ks[8]),
        "w_up": nrm(ks[9], (DEPTH, D_MODEL, 2 * D_FF), D_MODEL),
        "conv_ffn_w": nrm(ks[10], (DEPTH, FFN_CONV_K, 2 * D_FF), FFN_CONV_K),
        "w_down": nrm(ks[11], (DEPTH, D_FF, D_MODEL), D_FF),
        "norm_ffn_post": gain(ks[12]),
    }


def reference(x, norm_mix_pre, w_in, conv_mix_w, w_att_branch, w_conv_branch, w_out,
              norm_mix_post, norm_ffn_pre, w_up, conv_ffn_w, w_down, norm_ffn_post):
    b, s, _ = x.shape
    for l in range(DEPTH):
        h = rms_norm(x, norm_mix_pre[l])
        proj = h @ w_in[l]
        q, k, v, cb, cc, cx, g_att, g_conv = jnp.split(proj, IN_SPLITS, axis=-1)

        def heads(t):
            return t.reshape(b, s, ATT_HEADS, HEAD_DIM).transpose(0, 2, 1, 3)

        o = stick_breaking_attention(heads(q), heads(k), heads(v))
        y_att = o.transpose(0, 2, 1, 3).reshape(b, s, ATT_WIDTH) @ w_att_branch[l]

        y_conv = (cb * causal_dwconv(cc * cx, conv_mix_w[l])) @ w_conv_branch[l]

        merged = jax.nn.sigmoid(g_att) * y_att + jax.nn.sigmoid(g_conv) * y_conv
        x = x + rms_norm(merged @ w_out[l], norm_mix_post[l])

        h = rms_norm(x, norm_ffn_pre[l])
        u = causal_dwconv(h @ w_up[l], conv_ffn_w[l])
        a, g = jnp.split(u, 2, axis=-1)
        f = (jax.nn.gelu(g, approximate=True) * a) @ w_down[l]
        x = x + rms_norm(f, norm_ffn_post[l])
    return x
```

```python
import contextlib
import numpy as np
import concourse.bass as bass
import concourse.mybir as mybir
from concourse.bass_utils import run_bass_kernel_spmd

F32 = mybir.dt.float32
BF16 = mybir.dt.bfloat16
AF = mybir.ActivationFunctionType
ALU = mybir.AluOpType

P = 128
D = 1024
KC = 8
ST = 512
ATT = 512
DFF = 2816
FC = 22
INC = 5120
UPC = 5632
NSLOT = 3
EPS = 1e-6
DEPTH = 4
SEQ = 4096

C_Q, C_K, C_V, C_CB, C_CC, C_CX, C_GA, C_GC = 0, 512, 1024, 1536, 2048, 2560, 3072, 4096

def gidx(kind, l, c):
    return (kind * DEPTH + l) * 8 + c
PP_CM = 4 * DEPTH * 8
def cmidx(l, i, c):
    return PP_CM + (l * 3 + i) * 4 + c
PP_CF = PP_CM + DEPTH * 3 * 4
def cfidx(l, i, ch):
    return PP_CF + (l * 3 + i) * 44 + ch
NPP = PP_CF + DEPTH * 3 * 44

import os as _os0
SAFE_SAME_ENGINE = _os0.environ.get("KSAFE", "0") == "1"
MARKS = []


class Op:
    __slots__ = ("eng", "fn", "reads", "writes", "dkey", "deps", "signal", "tok", "name")


class Sched:
    def __init__(self):
        self.ops = []
        self.inserts = []

    def mk(self, eng, fn, reads=(), writes=(), dkey=None, name=""):
        op = Op()
        op.eng = eng
        op.fn = fn
        op.reads = tuple(reads)
        op.writes = tuple(writes)
        op.dkey = dkey
        op.name = name
        op.signal = False
        op.tok = None
        return op

    def add(self, eng, fn, reads=(), writes=(), dkey=None, name=""):
        op = self.mk(eng, fn, reads, writes, dkey, name)
        self.ops.append(op)
        return op

    def insert_at(self, pos, op):
        self.inserts.append((pos, op))

    def finalize(self):
        ins = sorted(enumerate(self.inserts), key=lambda t: (t[1][0], t[0]))
        out = []
        ii = 0
        for i, op in enumerate(self.ops):
            while ii < len(ins) and ins[ii][1][0] <= i:
                out.append(ins[ii][1][1])
                ii += 1
            out.append(op)
        while ii < len(ins):
            out.append(ins[ii][1][1])
            ii += 1
        self.ops = out

    def analyze(self):
        last_w = {}
        readers = {}
        for op in self.ops:
            deps = {}
            for k in op.reads:
                w = last_w.get(k)
                if w is not None:
                    deps[id(w)] = (w, True)
            for k in op.writes:
                w = last_w.get(k)
                if w is not None:
                    deps[id(w)] = (w, True)
                for r in readers.get(k, ()):
                    if id(r) not in deps:
                        deps[id(r)] = (r, False)
            deps.pop(id(op), None)
            op.deps = list(deps.values())
            for k in op.reads:
                readers.setdefault(k, []).append(op)
            for k in op.writes:
                last_w[k] = op
                readers[k] = []
        for op in self.ops:
            keep = []
            for d, hard in op.deps:
                if d.dkey is None and d.eng == op.eng and op.dkey is None:
                    if d.eng == "pe" or not hard or not SAFE_SAME_ENGINE:
                        continue
                keep.append(d)
            op.deps = keep
            for d in keep:
                d.signal = True


def pack_params(inp, NL):
    pp = np.zeros((P, NPP), np.float32)
    kinds = ["norm_mix_pre", "norm_mix_post", "norm_ffn_pre", "norm_ffn_post"]
    for k, nm in enumerate(kinds):
        a = np.asarray(inp[nm], np.float32)
        for l in range(NL):
            pp[:, gidx(k, l, 0):gidx(k, l, 0) + 8] = a[l].reshape(8, P).T
    cm = np.asarray(inp["conv_mix_w"], np.float32)
    cf = np.asarray(inp["conv_ffn_w"], np.float32)
    for l in range(NL):
        for i in range(3):
            pp[:, cmidx(l, i, 0):cmidx(l, i, 0) + 4] = cm[l, i].reshape(4, P).T
            pp[:, cfidx(l, i, 0):cfidx(l, i, 0) + 44] = cf[l, i].reshape(44, P).T
    return pp


def const_mats():
    j = np.arange(P)[:, None]
    s = np.arange(P)[None, :]
    tri = (j >= s).astype(np.float32)
    msk = (j < s).astype(np.float32)
    return np.concatenate([tri, msk], axis=1)


WNAMES = ("w_in", "w_att_branch", "w_conv_branch", "w_out", "w_up", "w_down")
WSHAPE = {"w_in": (D, INC), "w_att_branch": (ATT, D), "w_conv_branch": (ATT, D),
          "w_out": (D, D), "w_up": (D, UPC), "w_down": (DFF, D)}
WPIECES = {"w_in": 8, "w_att_branch": 1, "w_conv_branch": 1, "w_out": 2, "w_up": 8, "w_down": 8}


NZ, NSP, NW = 4, 4, 3
INTERLEAVE = True


def build_program(NL, NS):
    S_TOK = NS * ST
    nc = bass.Bass("TRN2", target_bir_lowering=False)
    sch = Sched()

    xT_d = nc.dram_tensor("xT", [D, S_TOK], F32, kind="ExternalInput").ap()
    out_d = nc.dram_tensor("outT", [D, S_TOK], F32, kind="ExternalOutput").ap()
    pp_d = nc.dram_tensor("pp", [P, NPP], F32, kind="ExternalInput").ap()
    cm_d = nc.dram_tensor("cmat", [P, 2 * P], F32, kind="ExternalInput").ap()
    w_d = {}
    wb_d = {}
    for nm in WNAMES:
        K, N = WSHAPE[nm]
        w_d[nm] = nc.dram_tensor(nm, [NL, K, N], F32, kind="ExternalInput").ap()
        wb_d[nm] = nc.dram_tensor("bf_" + nm, [NL, K, N], BF16, kind="Internal").ap()
    xs_d = [nc.dram_tensor(f"xs{i}", [D, S_TOK], F32, kind="Internal").ap() for i in range(2)]

    es = contextlib.ExitStack()
    with es:
        def sb(name, shape, dt):
            return es.enter_context(nc.sbuf_tensor(name, shape, dt))

        kT = sb("kT", [P, 4, S_TOK], BF16)
        vS = sb("vS", [P, 4 * NS, ATT], BF16)
        xT = sb("xTs", [P, KC, ST], F32)
        yT = sb("yTs", [P, KC, ST], F32)
        hT = [sb(f"hT{i}", [P, KC, ST], BF16) for i in range(2)]
        fT = sb("fT", [P, FC, ST], BF16)
        qT = sb("qT", [P, 4, ST], BF16)
        oT = [sb(f"oT{i}", [P, 4, ST], BF16) for i in range(2)]
        cvT = [sb(f"cvT{i}", [P, 4, ST], BF16) for i in range(2)]
        wsl = [sb(f"wsl{i}", [P, 8 * 512], BF16) for i in range(NSLOT)]
        NT32 = 4
        t32 = [sb(f"t32_{i}", [P, ST + 2], F32) for i in range(NT32)]
        NT16 = 2
        t16 = [sb(f"t16_{i}", [P, ST], BF16) for i in range(NT16)]
        aE = [sb(f"aE{i}", [P, ST], F32) for i in range(2)]
        aSP = [sb(f"aSP{i}", [P, ST], BF16) for i in range(NSP)]
        aW = [sb(f"aW{i}", [P, ST], BF16) for i in range(NW)]
        aR = [sb(f"aR{i}", [P, ST], BF16) for i in range(2)]
        rstd = sb("rstd", [P, ST], F32)
        pp = sb("pp_s", [P, NPP], F32)
        triI = sb("triI", [P, P], BF16)
        msk = sb("msk", [P, P], BF16)
        onesD = sb("onesD", [P, P], BF16)
        ones1 = sb("ones1", [P, P], BF16)
        halo_m = sb("halo_m", [P, 4, 2], F32)
        halo_f = sb("halo_f", [P, 44, 2], F32)
        ps = [es.enter_context(nc.psum_tensor(f"ps{i}", [P, ST], F32)) for i in range(8)]

        ZB = TB = (0, 1, 2, 3)
        OB = (4, 5)
        RING_D2 = (6, 7)
        RING_D1 = (0, 1, 2, 3, 6, 7)
        ring = {"ps": 0, "t32": 0, "t16": 0, "set": RING_D2}
        cur = {"grp": None, "gid": 0}

        def psn():
            rs = ring["set"]
            i = ring["ps"] % len(rs)
            ring["ps"] = (i + 1) % len(rs)
            return rs[i]

        def t32n():
            i = ring["t32"]
            ring["t32"] = (i + 1) % NT32
            if ring.get("hold") == i:
                i = ring["t32"]
                ring["t32"] = (i + 1) % NT32
            return i

        def t16n():
            i = ring["t16"]
            ring["t16"] = (i + 1) % NT16
            return i

        def ppc(i):
            return pp[:, i:i + 1]

        def add(eng, fn, reads=(), writes=(), dkey=None):
            op = sch.add(eng, fn, reads, writes, dkey)
            op.name = cur["grp"]
            return op

        add("sp", lambda e: e.dma_start(out=pp[:, :], in_=pp_d[:, :]), writes=["pp"], dkey="cst0")
        add("sp", lambda e: e.dma_start(out=yT[:, 0, 0:2 * P], in_=cm_d[:, :]), writes=[("yT", 0)], dkey="cst1")
        add("dve", lambda e: e.tensor_copy(out=triI[:, :], in_=yT[:, 0, 0:P]), reads=[("yT", 0)], writes=["triI"])
        add("dve", lambda e: e.tensor_copy(out=msk[:, :], in_=yT[:, 0, P:2 * P]), reads=[("yT", 0)], writes=["msk"])
        add("dve", lambda e: e.memset(onesD[:, :], 1.0 / D), writes=["onesD"])
        add("dve", lambda e: e.memset(ones1[:, :], 1.0), writes=["ones1"])

        def conv_ops(l):
            ops = []
            for nm in WNAMES:
                K, N = WSHAPE[nm]
                npc = WPIECES[nm]
                rows = K // npc
                for pc in range(npc):
                    def fn(e, nm=nm, l=l, r0=pc * rows, r1=(pc + 1) * rows):
                        return e.dma_start(out=wb_d[nm][l, r0:r1, :], in_=w_d[nm][l, r0:r1, :])
                    ops.append(sch.mk("pool", fn, writes=[("wb", nm, l, pc)], dkey=f"cv_{nm}_{l}"))
            return ops

        def wb_keys(nm, l):
            return [("wb", nm, l, pc) for pc in range(WPIECES[nm])]

        wstate = {"n": 0, "loads": []}

        def wtile(nm, l, kc0, nkc, col0, ncols):
            i = wstate["n"]
            wstate["n"] = i + 1
            slot = i % NSLOT
            mk = sch.add("pe", None)
            mk.name = ("wmark", i)
            view = wsl[slot][:, 0:nkc * ncols].rearrange("p (k n) -> p k n", k=nkc)
            src = wb_d[nm].rearrange("l (kc p) n -> l p kc n", p=P)[l, :, kc0:kc0 + nkc, col0:col0 + ncols]

            def fn(e, view=view, src=src):
                return e.dma_start(out=view, in_=src)
            op = sch.mk("sp", fn, reads=wb_keys(nm, l), writes=[("w", slot)], dkey=f"wsl{slot}")
            wstate["loads"].append(op)
            return view, ("w", slot)

        def mm_group(out_ap, pairs, reads, writes, start=True, stop=True, nocheck=False):
            def fn(e):
                ins = None
                n = len(pairs)
                for i, (a, b) in enumerate(pairs):
                    if nocheck:
                        ins = e.matmul(out_ap, a, b, start=(start and i == 0), stop=(stop and i == n - 1),
                                       skip_group_check=True)
                    else:
                        ins = e.matmul(out_ap, a, b, start=(start and i == 0), stop=(stop and i == n - 1))
                return ins
            add("pe", fn, reads=reads, writes=writes)

        def act(out, in_, func, reads, writes, scale=None, bias=None):
            kw = {}
            if scale is not None:
                kw["scale"] = scale
            if bias is not None:
                kw["bias"] = bias
            add("act", lambda e: e.activation(out=out, in_=in_, func=func, **kw), reads=reads, writes=writes)

        def v_copy(eng, out, in_, reads, writes):
            add(eng, lambda e: e.tensor_copy(out=out, in_=in_), reads=reads, writes=writes)

        def v_tt(eng, out, a, b, op, reads, writes):
            add(eng, lambda e: e.tensor_tensor(out=out, in0=a, in1=b, op=op), reads=reads, writes=writes)

        def v_ts(eng, out, a, s1, op0, reads, writes):
            add(eng, lambda e: e.tensor_scalar(out=out, in0=a, scalar1=s1, scalar2=None, op0=op0),
                reads=reads, writes=writes)

        def dve_stt(out, in0, scalar, in1, op0, op1, reads, writes):
            add("dve", lambda e: e.scalar_tensor_tensor(out=out, in0=in0, scalar=scalar, in1=in1, op0=op0, op1=op1),
                reads=reads, writes=writes)

        ssst = {}

        def rstd_from_ss():
            q = ssst["q"]
            b = psn()
            mm_group(ps[b][:, :], [(onesD[:, :], t16[q][:, :])], reads=[("t16", q), "onesD"], writes=[("ps", b)])
            t = t32n()
            act(t32[t][:, 0:ST], ps[b][:, :], AF.Ln, reads=[("ps", b)], writes=[("t32", t)], bias=EPS)
            act(rstd[:, :], t32[t][:, 0:ST], AF.Exp, reads=[("t32", t)], writes=["rstd"], scale=-0.5)

        def ss_accum(oc, src_ap, src_key):
            if oc == 0:
                ssst["acc"] = t32n()
                a = ssst["acc"]
                ring["hold"] = a
                act(t32[a][:, 0:ST], src_ap, AF.Square, reads=[src_key], writes=[("t32", a)])
                return
            a = ssst["acc"]
            t = t32n()
            act(t32[t][:, 0:ST], src_ap, AF.Square, reads=[src_key], writes=[("t32", t)])
            if oc < KC - 1:
                v_tt("dve", t32[a][:, 0:ST], t32[a][:, 0:ST], t32[t][:, 0:ST], ALU.add,
                     reads=[("t32", a), ("t32", t)], writes=[("t32", a)])
            else:
                q = t16n()
                ssst["q"] = q
                v_tt("dve", t16[q][:, :], t32[a][:, 0:ST], t32[t][:, 0:ST], ALU.add,
                     reads=[("t32", a), ("t32", t)], writes=[("t16", q)])
                ring["hold"] = None

        def pre_norm(kind, l, hbuf, hk):
            for c in range(KC):
                ss_accum(c, xT[:, c, :], ("xT", c))
            rstd_from_ss()
            for c in range(KC):
                dve_stt(hbuf[:, c, :], xT[:, c, :], ppc(gidx(kind, l, c)), rstd[:, :], ALU.mult, ALU.mult,
                        reads=[("xT", c), "rstd", "pp"], writes=[(hk, c)])

        def post_norm_residual(kind, l):
            rstd_from_ss()
            for c in range(KC):
                t = t32n()
                dve_stt(t32[t][:, 0:ST], yT[:, c, :], ppc(gidx(kind, l, c)), rstd[:, :], ALU.mult, ALU.mult,
                        reads=[("yT", c), "rstd", "pp"], writes=[("t32", t)])
                v_tt("pool" if c % 2 == 0 else "dve", xT[:, c, :], xT[:, c, :], t32[t][:, 0:ST], ALU.add,
                     reads=[("xT", c), ("t32", t)], writes=[("xT", c)])

        def conv3(src_ps, src_key, halo, hkey, wi, out_ap, out_key, mul_ap=None, mul_key=None):
            t = t32n()
            ub = t32[t]
            if mul_ap is None:
                act(ub[:, 2:ST + 2], src_ps, AF.Copy, reads=[src_key], writes=[("t32", t)])
            else:
                v_tt("dve", ub[:, 2:ST + 2], src_ps, mul_ap, ALU.mult, reads=[src_key, mul_key], writes=[("t32", t)])
            v_copy("dve", ub[:, 0:2], halo, reads=[hkey], writes=[("t32", t)])
            v_copy("dve", halo, ub[:, ST:ST + 2], reads=[("t32", t)], writes=[hkey])
            act(out_ap, ub[:, 0:ST], AF.Copy, reads=[("t32", t), "pp"], writes=[out_key], scale=ppc(wi[0]))
            dve_stt(out_ap, ub[:, 1:ST + 1], ppc(wi[1]), out_ap, ALU.mult, ALU.add,
                    reads=[("t32", t), "pp", out_key], writes=[out_key])
            dve_stt(out_ap, ub[:, 2:ST + 2], ppc(wi[2]), out_ap, ALU.mult, ALU.add,
                    reads=[("t32", t), "pp", out_key], writes=[out_key])

        def proj_fm(wt, wkey, j, rhs_fn, rhs_keys, nkc):
            b = psn()
            mm_group(ps[b][:, :], [(wt[:, kc, j * P:(j + 1) * P], rhs_fn(kc)) for kc in range(nkc)],
                     reads=[wkey] + list(rhs_keys), writes=[("ps", b)])
            return b

        def src_of(l):
            return xT_d if l == 0 else xs_d[(l - 1) % 2]

        def load_x(l, s):
            srcv = src_of(l).rearrange("(c p) t -> p c t", p=P)[:, :, s * ST:(s + 1) * ST]
            add("sp", lambda e: e.dma_start(out=xT[:, :, :], in_=srcv),
                reads=([("xs", (l - 1) % 2, s)] if l > 0 else []), writes=[("xT", c) for c in range(KC)],
                dkey="xld")

        def stage_d1(l, s, p):
            t0, t1 = s * ST, (s + 1) * ST
            hb, hk = hT[p], "hT%d" % p
            hkeys = [(hk, c) for c in range(KC)]
            hfn = lambda kc: hb[:, kc, :]
            load_x(l, s)
            if s == 0:
                add("dve", lambda e: e.memset(halo_m[:, :, :], 0.0), writes=[("hm", c) for c in range(4)])
            pre_norm(0, l, hb, hk)
            wt, wk = wtile("w_in", l, 0, KC, C_Q, 512)
            for j in range(4):
                b = proj_fm(wt, wk, j, hfn, hkeys, KC)
                act(qT[:, j, :], ps[b][:, :], AF.Copy, reads=[("ps", b)], writes=[("qT", j)], scale=-0.125)
            wt, wk = wtile("w_in", l, 0, KC, C_K, 512)
            for j in range(4):
                b = proj_fm(wt, wk, j, hfn, hkeys, KC)
                v_copy("dve", kT[:, j, t0:t1], ps[b][:, :], reads=[("ps", b)], writes=[("kT", j, s)])
            wt, wk = wtile("w_in", l, 0, KC, C_V, 512)
            for tb in range(4):
                b = psn()
                mm_group(ps[b][:, :], [(hb[:, kc, tb * P:(tb + 1) * P], wt[:, kc, :]) for kc in range(KC)],
                         reads=[wk] + hkeys, writes=[("ps", b)])
                if tb % 2 == 0:
                    act(vS[:, 4 * s + tb, :], ps[b][:, :], AF.Copy, reads=[("ps", b)], writes=[("vS", 4 * s + tb)])
                else:
                    v_copy("dve", vS[:, 4 * s + tb, :], ps[b][:, :], reads=[("ps", b)], writes=[("vS", 4 * s + tb)])
            wt, wk = wtile("w_in", l, 0, KC, C_CC, 512)
            for j in range(4):
                b = proj_fm(wt, wk, j, hfn, hkeys, KC)
                act(yT[:, j, :], ps[b][:, :], AF.Copy, reads=[("ps", b)], writes=[("yT", j)])
            wt, wk = wtile("w_in", l, 0, KC, C_CX, 512)
            for j in range(4):
                b = proj_fm(wt, wk, j, hfn, hkeys, KC)
                conv3(ps[b][:, :], ("ps", b), halo_m[:, j, :], ("hm", j),
                      [cmidx(l, i, j) for i in range(3)], yT[:, j, :], ("yT", j),
                      mul_ap=yT[:, j, :], mul_key=("yT", j))
            wt, wk = wtile("w_in", l, 0, KC, C_CB, 512)
            for j in range(4):
                b = proj_fm(wt, wk, j, hfn, hkeys, KC)
                v_tt("dve", cvT[p][:, j, :], ps[b][:, :], yT[:, j, :], ALU.mult,
                     reads=[("ps", b), ("yT", j)], writes=[("cvT%d" % p, j)])

        def stage_attn(l, s, p):
            steps = []
            for h in range(8):
                kbs = list(range(4 * s + 3, -1, -1))
                for i, kb in enumerate(kbs):
                    j = kb - 4 * s
                    steps.append(dict(h=h, kb=kb, c0=(j * P if j >= 0 else 0), first=(i == 0),
                                      last=(i == len(kbs) - 1), diag=(j >= 0)))
            n = len(steps)

            def common(st):
                h = st["h"]
                return h, h // 2, (h % 2) * 64, st["kb"], st["c0"]

            def stageA_pe(i):
                st = steps[i]
                h, c, p0, kb, c0 = common(st)
                zb = ZB[i % NZ]
                mm_group(ps[zb][:, c0:ST], [(kT[p0:p0 + 64, c, kb * P:(kb + 1) * P], qT[p0:p0 + 64, c, c0:ST])],
                         reads=[("kT", c, kb // 4), ("qT", c)], writes=[("ps", zb)], start=True, stop=True)

            def stageA_act(i):
                st = steps[i]
                h, c, p0, kb, c0 = common(st)
                zb = ZB[i % NZ]
                eb = i % 2
                sp = i % NSP
                if st["first"]:
                    add("dve", (lambda r: lambda e: e.memset(aR[r][:, :], 0.0))(h % 2), writes=[("aR", h % 2)])
                act(aE[eb][:, c0:ST], ps[zb][:, c0:ST], AF.Exp, reads=[("ps", zb)], writes=[("aE", eb)], scale=-1.0)
                act(aSP[sp][:, c0:ST], aE[eb][:, c0:ST], AF.Ln, reads=[("aE", eb)], writes=[("aSP", sp)], bias=1.0)
                if st["diag"]:
                    v_tt("dve", aSP[sp][:, c0:c0 + P], aSP[sp][:, c0:c0 + P], msk[:, :], ALU.mult,
                         reads=[("aSP", sp), "msk"], writes=[("aSP", sp)])

            def stageB_pe(i):
                st = steps[i]
                h, c, p0, kb, c0 = common(st)
                tb = TB[i % NZ]
                sp = i % NSP
                r = h % 2
                pairs = [(triI[:, :], aSP[sp][:, c0:ST])]
                reads = ["triI", ("aSP", sp)]
                if not st["first"]:
                    pairs.append((ones1[:, :], aR[r][:, c0:ST]))
                    reads += ["ones1", ("aR", r)]
                mm_group(ps[tb][:, c0:ST], pairs, reads=reads, writes=[("ps", tb)], start=False, stop=True,
                         nocheck=True)
                if not st["last"]:
                    v_tt("dve", aR[r][:, c0:ST], aR[r][:, c0:ST], aSP[sp][:, c0:ST], ALU.add,
                         reads=[("aR", r), ("aSP", sp)], writes=[("aR", r)])

            def stageB_act(i):
                st = steps[i]
                h, c, p0, kb, c0 = common(st)
                tb = TB[i % NZ]
                wb, wk = aW[i % NW], ("aW", i % NW)
                act(wb[:, c0:ST], ps[tb][:, c0:ST], AF.Exp, reads=[("ps", tb)], writes=[wk], scale=-1.0)
                if st["diag"]:
                    v_tt("dve", wb[:, c0:c0 + P], wb[:, c0:c0 + P], msk[:, :], ALU.mult, reads=[wk, "msk"], writes=[wk])

            def stageC(i):
                st = steps[i]
                h, c, p0, kb, c0 = common(st)
                ob = OB[h % 2]
                wb, wk = aW[i % NW], ("aW", i % NW)
                mm_group(ps[ob][p0:p0 + 64, c0:ST], [(vS[:, kb, h * 64:(h + 1) * 64], wb[:, c0:ST])],
                         reads=[("vS", kb), wk], writes=[("ps", ob)], start=st["first"], stop=st["last"],
                         nocheck=True)
                if st["last"]:
                    v_copy("dve", oT[p][p0:p0 + 64, c, :], ps[ob][p0:p0 + 64, :],
                           reads=[("ps", ob)], writes=[("oT%d" % p, c)])

            for t in range(n + 4):
                if t < n:
                    stageA_pe(t)
                if 0 <= t - 1 < n:
                    stageA_act(t - 1)
                if 0 <= t - 2 < n:
                    stageB_pe(t - 2)
                if 0 <= t - 3 < n:
                    stageB_act(t - 3)
                if 0 <= t - 4 < n:
                    stageC(t - 4)

        def sticky():
            cur["gid"] += 1
            cur["grp"] = cur["gid"]

        def unsticky():
            cur["grp"] = None

        def stage_d2(l, s, p):
            t0, t1 = s * ST, (s + 1) * ST
            hb, hk = hT[p], "hT%d" % p
            hkeys = [(hk, c) for c in range(KC)]
            hfn = lambda kc: hb[:, kc, :]
            ok_ = "oT%d" % p
            ck_ = "cvT%d" % p
            ofn = lambda kc: oT[p][:, kc, :]
            okeys = [(ok_, kc) for kc in range(4)]
            cfn = lambda kc: cvT[p][:, kc, :]
            ckeys = [(ck_, kc) for kc in range(4)]
            load_x(l, s)
            if s == 0:
                add("dve", lambda e: e.memset(halo_f[:, :, :], 0.0), writes=[("hf", c) for c in range(44)])

            def gate_sig(b, j):
                act(yT[:, j, :], ps[b][:, :], AF.Exp, reads=[("ps", b)], writes=[("yT", j)], scale=-1.0)
                act(yT[:, j, :], yT[:, j, :], AF.Ln, reads=[("yT", j)], writes=[("yT", j)], bias=1.0)
                act(yT[:, j, :], yT[:, j, :], AF.Exp, reads=[("yT", j)], writes=[("yT", j)], scale=-1.0)

            for half in range(2):
                wt, wk = wtile("w_in", l, 0, KC, C_GA + half * 512, 512)
                for j in range(4):
                    b = proj_fm(wt, wk, j, hfn, hkeys, KC)
                    gate_sig(b, j)
                wt, wk = wtile("w_att_branch", l, 0, 4, half * 512, 512)
                for j in range(4):
                    b = proj_fm(wt, wk, j, ofn, okeys, 4)
                    v_tt("dve", yT[:, j, :], ps[b][:, :], yT[:, j, :], ALU.mult,
                         reads=[("ps", b), ("yT", j)], writes=[("yT", j)])
                wt, wk = wtile("w_in", l, 0, KC, C_GC + half * 512, 512)
                for j in range(4):
                    b = proj_fm(wt, wk, j, hfn, hkeys, KC)
                    gate_sig(b, 4 + j)
                wt, wk = wtile("w_conv_branch", l, 0, 4, half * 512, 512)
                for j in range(4):
                    b = proj_fm(wt, wk, j, cfn, ckeys, 4)
                    v_tt("dve", yT[:, 4 + j, :], ps[b][:, :], yT[:, 4 + j, :], ALU.mult,
                         reads=[("ps", b), ("yT", 4 + j)], writes=[("yT", 4 + j)])
                    oc = half * 4 + j
                    v_tt("pool", fT[:, 12 + oc, :], yT[:, j, :], yT[:, 4 + j, :], ALU.add,
                         reads=[("yT", j), ("yT", 4 + j)], writes=[("fT", 12 + oc)])
            mfn = lambda kc: fT[:, 12 + kc, :]
            mkeys = [("fT", 12 + kc) for kc in range(KC)]
            for half in range(2):
                wt, wk = wtile("w_out", l, 0, KC, half * 512, 512)
                for j in range(4):
                    oc = half * 4 + j
                    b = proj_fm(wt, wk, j, mfn, mkeys, KC)
                    v_copy("dve", yT[:, oc, :], ps[b][:, :], reads=[("ps", b)], writes=[("yT", oc)])
                    ss_accum(oc, yT[:, oc, :], ("yT", oc))
            post_norm_residual(1, l)
            pre_norm(2, l, hb, hk)
            for grp in range(6):
                nj = 4 if grp < 5 else 2
                wt, wk = wtile("w_up", l, 0, KC, grp * 512, nj * P)
                for j in range(nj):
                    ch = grp * 4 + j
                    b = proj_fm(wt, wk, j, hfn, hkeys, KC)
                    conv3(ps[b][:, :], ("ps", b), halo_f[:, ch, :], ("hf", ch),
                          [cfidx(l, i, ch) for i in range(3)], yT[:, j, :], ("yT", j))
                wt, wk = wtile("w_up", l, 0, KC, DFF + grp * 512, nj * P)
                for j in range(nj):
                    ch = FC + grp * 4 + j
                    b = proj_fm(wt, wk, j, hfn, hkeys, KC)
                    conv3(ps[b][:, :], ("ps", b), halo_f[:, ch, :], ("hf", ch),
                          [cfidx(l, i, ch) for i in range(3)], yT[:, 4 + j, :], ("yT", 4 + j))
                sticky()
                for j in range(nj):
                    act(yT[:, 4 + j, :], yT[:, 4 + j, :], AF.Gelu_apprx_tanh,
                        reads=[("yT", 4 + j)], writes=[("yT", 4 + j)])
                unsticky()
                for j in range(nj):
                    v_tt("pool" if j % 2 == 0 else "dve", fT[:, grp * 4 + j, :], yT[:, 4 + j, :], yT[:, j, :], ALU.mult,
                         reads=[("yT", j), ("yT", 4 + j)], writes=[("fT", grp * 4 + j)])
            for cg in range(4):
                banks = [psn(), psn()]
                for kh in range(2):
                    wt, wk = wtile("w_down", l, kh * 11, 11, cg * 256, 256)
                    for j in range(2):
                        b = banks[j]
                        mm_group(ps[b][:, :],
                                 [(wt[:, kk, j * P:(j + 1) * P], fT[:, kh * 11 + kk, :]) for kk in range(11)],
                                 reads=[wk] + [("fT", kh * 11 + kk) for kk in range(11)], writes=[("ps", b)],
                                 start=(kh == 0), stop=(kh == 1))
                for j in range(2):
                    oc = cg * 2 + j
                    b = banks[j]
                    v_copy("dve", yT[:, oc, :], ps[b][:, :], reads=[("ps", b)], writes=[("yT", oc)])
                    ss_accum(oc, yT[:, oc, :], ("yT", oc))
            post_norm_residual(3, l)
            last = (l == NL - 1)
            dst = out_d if last else xs_d[l % 2]
            dstv = dst.rearrange("(c p) t -> p c t", p=P)[:, :, t0:t1]
            add("sp", lambda e: e.dma_start(out=dstv, in_=xT[:, :, :]),
                reads=[("xT", c) for c in range(KC)],
                writes=[("out", s) if last else ("xs", l % 2, s)], dkey="xst")

        def gen(fn, *a):
            saved = sch.ops
            sch.ops = []
            fn(*a)
            out = sch.ops
            sch.ops = saved
            return out

        class _FakePE:
            def __init__(self):
                self.n = 0

            def matmul(self, *a, **k):
                self.n += 1
                return self

        def op_cost(op):
            if op.fn is None:
                return 0.0
            if op.eng == "pe":
                f = _FakePE()
                op.fn(f)
                return 0.25 * f.n
            if op.eng == "act":
                return 0.62
            if op.eng == "dve":
                return 0.6
            if op.eng == "pool":
                return 1.45
            return 0.1

        def stream_deps(ops):
            last_w, readers, deps = {}, {}, []
            for i, op in enumerate(ops):
                d = set()
                for k in op.reads:
                    if k in last_w:
                        d.add(last_w[k])
                for k in op.writes:
                    if k in last_w:
                        d.add(last_w[k])
                    d.update(readers.get(k, ()))
                d.discard(i)
                deps.append(d)
                for k in op.reads:
                    readers.setdefault(k, []).append(i)
                for k in op.writes:
                    last_w[k] = i
                    readers[k] = []
            return deps

        def merge(A, B):
            streams = [A, B]
            deps = [stream_deps(A), stream_deps(B)]
            fin = [[0.0] * len(A), [0.0] * len(B)]
            idx = [0, 0]
            free = {"pe": 0.0, "act": 0.0, "dve": 0.0, "pool": 0.0, "sp": 0.0}
            out = []

            def est(k):
                i = idx[k]
                op = streams[k][i]
                ready = max([fin[k][j] for j in deps[k][i]], default=0.0)
                return max(ready, free[op.eng])

            def emit_one(k):
                i = idx[k]
                op = streams[k][i]
                st = est(k)
                c = op_cost(op)
                if op.dkey is not None:
                    free[op.eng] = st + 0.1
                    fin[k][i] = st + 4.0
                else:
                    free[op.eng] = st + c
                    fin[k][i] = st + c
                out.append(op)
                idx[k] += 1

            while idx[0] < len(A) or idx[1] < len(B):
                if idx[0] >= len(A):
                    k = 1
                elif idx[1] >= len(B):
                    k = 0
                else:
                    k = 0 if est(0) <= est(1) else 1
                g = streams[k][idx[k]].name
                emit_one(k)
                if g is not None and not isinstance(g, tuple):
                    while idx[k] < len(streams[k]) and streams[k][idx[k]].name == g:
                        emit_one(k)
            return out

        for op in conv_ops(0):
            sch.ops.append(op)
        base_mark = sch.add("pe", None)
        base_mark.name = ("wmark", -1)
        G = NL * NS
        pend_conv = []
        for g in range(G + 1):
            l, s = divmod(g, NS)
            if g < G and s == 0 and l + 1 < NL:
                pend_conv = conv_ops(l + 1)
            if g < G:
                per = (len(WNAMES) and (sum(WPIECES.values()) + NS - 1) // NS)
                for op in pend_conv[:per]:
                    sch.ops.append(op)
                pend_conv = pend_conv[per:]
                ring["set"] = RING_D1
                sch.ops.extend(gen(stage_d1, l, s, g % 2))
                ring["set"] = RING_D2
            A = gen(stage_attn, l, s, g % 2) if g < G else []
            B = []
            if g >= 1:
                lp, sp_ = divmod(g - 1, NS)
                B = gen(stage_d2, lp, sp_, (g - 1) % 2)
            if INTERLEAVE:
                sch.ops.extend(merge(A, B))
            else:
                sch.ops.extend(A)
                sch.ops.extend(B)
        sch.add("sp", None, reads=[("out", s) for s in range(NS)])

        before = {}
        for i, op in enumerate(wstate["loads"]):
            before.setdefault(max(i - (NSLOT - 1), -1), []).append(op)
        merged = []
        for op in sch.ops:
            if isinstance(op.name, tuple) and op.name[0] == "wmark":
                merged.extend(before.get(op.name[1], []))
                continue
            merged.append(op)
        sch.ops = merged
        sch.analyze()
        emit(nc, sch, es)
    return nc


def emit(nc, sch, es):
    engs = ("pe", "act", "dve", "pool", "sp")
    esem = {e: es.enter_context(nc.semaphore("sem_" + e)) for e in engs}
    dsem = {}
    for op in sch.ops:
        if op.dkey is not None and op.dkey not in dsem:
            dsem[op.dkey] = es.enter_context(nc.semaphore("d_" + op.dkey))
    cnt = {e: 0 for e in engs}
    dcnt = {k: 0 for k in dsem}
    for op in sch.ops:
        if op.dkey is not None:
            dcnt[op.dkey] += 16
            op.tok = (dsem[op.dkey], dcnt[op.dkey], op.dkey)
        elif op.signal:
            cnt[op.eng] += 1
            op.tok = (esem[op.eng], cnt[op.eng], op.eng)
    by_eng = {e: [] for e in engs}
    for op in sch.ops:
        by_eng[op.eng].append(op)

    def run(eng_name, e):
        known = {}
        for op in by_eng[eng_name]:
            need = {}
            for d in op.deps:
                sem, val, key = d.tok
                if known.get(key, 0) >= val:
                    continue
                if need.get(key, (None, 0))[1] < val:
                    need[key] = (sem, val)
            for key, (sem, val) in need.items():
                e.wait_ge(sem, val)
                known[key] = val
            if op.fn is None:
                continue
            ins = op.fn(e)
            if op.dkey is not None:
                ins.then_inc(op.tok[0], 16)
            elif op.signal:
                ins.then_inc(op.tok[0], 1)

    with nc.Block() as block:
        @block.tensor
        def _(e):
            run("pe", e)

        @block.scalar
        def _(e):
            run("act", e)

        @block.vector
        def _(e):
            run("dve", e)

        @block.gpsimd
        def _(e):
            run("pool", e)

        @block.sync
        def _(e):
            run("sp", e)


_CACHE = {}


def run_model(inputs, NL, NB, NS):
    key = (NL, NS)
    if key not in _CACHE:
        _CACHE[key] = build_program(NL, NS)
    nc = _CACHE[key]
    x = np.asarray(inputs["x"], np.float32)
    pp = pack_params(inputs, NL)
    cm = const_mats()
    shared = {"pp": pp, "cmat": cm}
    for nm in WNAMES:
        shared[nm] = np.ascontiguousarray(np.asarray(inputs[nm], np.float32)[:NL])
    in_maps = []
    for b in range(NB):
        m = dict(shared)
        m["xT"] = np.ascontiguousarray(x[b].T)
        in_maps.append(m)
    res = run_bass_kernel_spmd(nc, in_maps, core_ids=list(range(NB)))
    out = np.stack([np.asarray(r["outT"], np.float32).T for r in res.results], axis=0)
    return np.ascontiguousarray(out)


def kernel(**inputs):
    return run_model(inputs, DEPTH, 8, SEQ // ST)
```

```python
import contextlib
import numpy as np
import concourse.bass as bass
import concourse.mybir as mybir
from concourse.bass_utils import run_bass_kernel_spmd

F32 = mybir.dt.float32
BF16 = mybir.dt.bfloat16
AF = mybir.ActivationFunctionType
ALU = mybir.AluOpType

P = 128
D = 1024
KC = 8
ST = 512
ATT = 512
DFF = 2816
FC = 22
INC = 5120
UPC = 5632
NSLOT = 3
EPS = 1e-6
DEPTH = 4
SEQ = 4096

C_Q, C_K, C_V, C_CB, C_CC, C_CX, C_GA, C_GC = 0, 512, 1024, 1536, 2048, 2560, 3072, 4096

def gidx(kind, l, c):
    return (kind * DEPTH + l) * 8 + c
PP_CM = 4 * DEPTH * 8
def cmidx(l, i, c):
    return PP_CM + (l * 3 + i) * 4 + c
PP_CF = PP_CM + DEPTH * 3 * 4
def cfidx(l, i, ch):
    return PP_CF + (l * 3 + i) * 44 + ch
NPP = PP_CF + DEPTH * 3 * 44

import os as _os0
SAFE_SAME_ENGINE = _os0.environ.get("KSAFE", "0") == "1"
MARKS = []


class Op:
    __slots__ = ("eng", "fn", "reads", "writes", "dkey", "deps", "signal", "tok", "name")


class Sched:
    def __init__(self):
        self.ops = []
        self.inserts = []

    def mk(self, eng, fn, reads=(), writes=(), dkey=None, name=""):
        op = Op()
        op.eng = eng
        op.fn = fn
        op.reads = tuple(reads)
        op.writes = tuple(writes)
        op.dkey = dkey
        op.name = name
        op.signal = False
        op.tok = None
        return op

    def add(self, eng, fn, reads=(), writes=(), dkey=None, name=""):
        op = self.mk(eng, fn, reads, writes, dkey, name)
        self.ops.append(op)
        return op

    def insert_at(self, pos, op):
        self.inserts.append((pos, op))

    def finalize(self):
        ins = sorted(enumerate(self.inserts), key=lambda t: (t[1][0], t[0]))
        out = []
        ii = 0
        for i, op in enumerate(self.ops):
            while ii < len(ins) and ins[ii][1][0] <= i:
                out.append(ins[ii][1][1])
                ii += 1
            out.append(op)
        while ii < len(ins):
            out.append(ins[ii][1][1])
            ii += 1
        self.ops = out

    def analyze(self):
        last_w = {}
        readers = {}
        for op in self.ops:
            deps = {}
            for k in op.reads:
                w = last_w.get(k)
                if w is not None:
                    deps[id(w)] = (w, True)
            for k in op.writes:
                w = last_w.get(k)
                if w is not None:
                    deps[id(w)] = (w, True)
                for r in readers.get(k, ()):
                    if id(r) not in deps:
                        deps[id(r)] = (r, False)
            deps.pop(id(op), None)
            op.deps = list(deps.values())
            for k in op.reads:
                readers.setdefault(k, []).append(op)
            for k in op.writes:
                last_w[k] = op
                readers[k] = []
        for op in self.ops:
            keep = []
            for d, hard in op.deps:
                if d.dkey is None and d.eng == op.eng and op.dkey is None:
                    if d.eng == "pe" or not hard or not SAFE_SAME_ENGINE:
                        continue
                keep.append(d)
            op.deps = keep
            for d in keep:
                d.signal = True


def pack_params(inp, NL):
    pp = np.zeros((P, NPP), np.float32)
    kinds = ["norm_mix_pre", "norm_mix_post", "norm_ffn_pre", "norm_ffn_post"]
    for k, nm in enumerate(kinds):
        a = np.asarray(inp[nm], np.float32)
        for l in range(NL):
            pp[:, gidx(k, l, 0):gidx(k, l, 0) + 8] = a[l].reshape(8, P).T
    cm = np.asarray(inp["conv_mix_w"], np.float32)
    cf = np.asarray(inp["conv_ffn_w"], np.float32)
    for l in range(NL):
        for i in range(3):
            pp[:, cmidx(l, i, 0):cmidx(l, i, 0) + 4] = cm[l, i].reshape(4, P).T
            pp[:, cfidx(l, i, 0):cfidx(l, i, 0) + 44] = cf[l, i].reshape(44, P).T
    return pp


def const_mats():
    j = np.arange(P)[:, None]
    s = np.arange(P)[None, :]
    tri = (j >= s).astype(np.float32)
    msk = (j < s).astype(np.float32)
    return np.concatenate([tri, msk], axis=1)


WNAMES = ("w_in", "w_att_branch", "w_conv_branch", "w_out", "w_up", "w_down")
WSHAPE = {"w_in": (D, INC), "w_att_branch": (ATT, D), "w_conv_branch": (ATT, D),
          "w_out": (D, D), "w_up": (D, UPC), "w_down": (DFF, D)}
WPIECES = {"w_in": 8, "w_att_branch": 1, "w_conv_branch": 1, "w_out": 2, "w_up": 8, "w_down": 8}


NZ, NSP, NW = 4, 4, 3
INTERLEAVE = True


def build_program(NL, NS):
    S_TOK = NS * ST
    nc = bass.Bass("TRN2", target_bir_lowering=False)
    sch = Sched()

    xT_d = nc.dram_tensor("xT", [D, S_TOK], F32, kind="ExternalInput").ap()
    out_d = nc.dram_tensor("outT", [D, S_TOK], F32, kind="ExternalOutput").ap()
    pp_d = nc.dram_tensor("pp", [P, NPP], F32, kind="ExternalInput").ap()
    cm_d = nc.dram_tensor("cmat", [P, 2 * P], F32, kind="ExternalInput").ap()
    w_d = {}
    wb_d = {}
    for nm in WNAMES:
        K, N = WSHAPE[nm]
        w_d[nm] = nc.dram_tensor(nm, [NL, K, N], F32, kind="ExternalInput").ap()
        wb_d[nm] = nc.dram_tensor("bf_" + nm, [NL, K, N], BF16, kind="Internal").ap()
    xs_d = [nc.dram_tensor(f"xs{i}", [D, S_TOK], F32, kind="Internal").ap() for i in range(2)]

    es = contextlib.ExitStack()
    with es:
        def sb(name, shape, dt):
            return es.enter_context(nc.sbuf_tensor(name, shape, dt))

        kT = sb("kT", [P, 4, S_TOK], BF16)
        vS = sb("vS", [P, 4 * NS, ATT], BF16)
        xT = sb("xTs", [P, KC, ST], F32)
        yT = sb("yTs", [P, KC, ST], F32)
        hT = [sb(f"hT{i}", [P, KC, ST], BF16) for i in range(2)]
        fT = sb("fT", [P, FC, ST], BF16)
        qT = sb("qT", [P, 4, ST], BF16)
        oT = [sb(f"oT{i}", [P, 4, ST], BF16) for i in range(2)]
        cvT = [sb(f"cvT{i}", [P, 4, ST], BF16) for i in range(2)]
        wsl = [sb(f"wsl{i}", [P, 8 * 512], BF16) for i in range(NSLOT)]
        NT32 = 4
        t32 = [sb(f"t32_{i}", [P, ST + 2], F32) for i in range(NT32)]
        NT16 = 2
        t16 = [sb(f"t16_{i}", [P, ST], BF16) for i in range(NT16)]
        aE = [sb(f"aE{i}", [P, ST], F32) for i in range(2)]
        aSP = [sb(f"aSP{i}", [P, ST], BF16) for i in range(NSP)]
        aW = [sb(f"aW{i}", [P, ST], BF16) for i in range(NW)]
        aR = [sb(f"aR{i}", [P, ST], BF16) for i in range(2)]
        rstd = sb("rstd", [P, ST], F32)
        pp = sb("pp_s", [P, NPP], F32)
        triI = sb("triI", [P, P], BF16)
        msk = sb("msk", [P, P], BF16)
        onesD = sb("onesD", [P, P], BF16)
        ones1 = sb("ones1", [P, P], BF16)
        halo_m = sb("halo_m", [P, 4, 2], F32)
        halo_f = sb("halo_f", [P, 44, 2], F32)
        ps = [es.enter_context(nc.psum_tensor(f"ps{i}", [P, ST], F32)) for i in range(8)]

        ZB = TB = (0, 1, 2, 3)
        OB = (4, 4)
        RING_D2 = (5, 6, 7)
        RING_D1 = (0, 1, 2, 3, 5, 6, 7)
        ring = {"ps": 0, "t32": 0, "t16": 0, "set": RING_D2}
        cur = {"grp": None, "gid": 0}

        def psn():
            rs = ring["set"]
            i = ring["ps"] % len(rs)
            ring["ps"] = (i + 1) % len(rs)
            return rs[i]

        def t32n():
            i = ring["t32"]
            ring["t32"] = (i + 1) % NT32
            if ring.get("hold") == i:
                i = ring["t32"]
                ring["t32"] = (i + 1) % NT32
            return i

        def t16n():
            i = ring["t16"]
            ring["t16"] = (i + 1) % NT16
            return i

        def ppc(i):
            return pp[:, i:i + 1]

        def add(eng, fn, reads=(), writes=(), dkey=None):
            op = sch.add(eng, fn, reads, writes, dkey)
            op.name = cur["grp"]
            return op

        add("sp", lambda e: e.dma_start(out=pp[:, :], in_=pp_d[:, :]), writes=["pp"], dkey="cst0")
        add("sp", lambda e: e.dma_start(out=yT[:, 0, 0:2 * P], in_=cm_d[:, :]), writes=[("yT", 0)], dkey="cst1")
        add("dve", lambda e: e.tensor_copy(out=triI[:, :], in_=yT[:, 0, 0:P]), reads=[("yT", 0)], writes=["triI"])
        add("dve", lambda e: e.tensor_copy(out=msk[:, :], in_=yT[:, 0, P:2 * P]), reads=[("yT", 0)], writes=["msk"])
        add("dve", lambda e: e.memset(onesD[:, :], 1.0 / D), writes=["onesD"])
        add("dve", lambda e: e.memset(ones1[:, :], 1.0), writes=["ones1"])

        def conv_ops(l):
            ops = []
            for nm in WNAMES:
                K, N = WSHAPE[nm]
                npc = WPIECES[nm]
                rows = K // npc
                for pc in range(npc):
                    def fn(e, nm=nm, l=l, r0=pc * rows, r1=(pc + 1) * rows):
                        return e.dma_start(out=wb_d[nm][l, r0:r1, :], in_=w_d[nm][l, r0:r1, :])
                    ops.append(sch.mk("pool", fn, writes=[("wb", nm, l, pc)], dkey=f"cv_{nm}_{l}"))
            return ops

        def wb_keys(nm, l):
            return [("wb", nm, l, pc) for pc in range(WPIECES[nm])]

        wstate = {"n": 0, "loads": []}

        def wtile(nm, l, kc0, nkc, col0, ncols):
            i = wstate["n"]
            wstate["n"] = i + 1
            slot = i % NSLOT
            mk = sch.add("pe", None)
            mk.name = ("wmark", i)
            view = wsl[slot][:, 0:nkc * ncols].rearrange("p (k n) -> p k n", k=nkc)
            src = wb_d[nm].rearrange("l (kc p) n -> l p kc n", p=P)[l, :, kc0:kc0 + nkc, col0:col0 + ncols]

            def fn(e, view=view, src=src):
                return e.dma_start(out=view, in_=src)
            op = sch.mk("sp", fn, reads=wb_keys(nm, l), writes=[("w", slot)], dkey=f"wsl{slot}")
            wstate["loads"].append(op)
            return view, ("w", slot)

        def mm_group(out_ap, pairs, reads, writes, start=True, stop=True, nocheck=False):
            def fn(e):
                ins = None
                n = len(pairs)
                for i, (a, b) in enumerate(pairs):
                    if nocheck:
                        ins = e.matmul(out_ap, a, b, start=(start and i == 0), stop=(stop and i == n - 1),
                                       skip_group_check=True)
                    else:
                        ins = e.matmul(out_ap, a, b, start=(start and i == 0), stop=(stop and i == n - 1))
                return ins
            add("pe", fn, reads=reads, writes=writes)

        def act(out, in_, func, reads, writes, scale=None, bias=None):
            kw = {}
            if scale is not None:
                kw["scale"] = scale
            if bias is not None:
                kw["bias"] = bias
            add("act", lambda e: e.activation(out=out, in_=in_, func=func, **kw), reads=reads, writes=writes)

        def v_copy(eng, out, in_, reads, writes):
            add(eng, lambda e: e.tensor_copy(out=out, in_=in_), reads=reads, writes=writes)

        def v_tt(eng, out, a, b, op, reads, writes):
            add(eng, lambda e: e.tensor_tensor(out=out, in0=a, in1=b, op=op), reads=reads, writes=writes)

        def v_ts(eng, out, a, s1, op0, reads, writes):
            add(eng, lambda e: e.tensor_scalar(out=out, in0=a, scalar1=s1, scalar2=None, op0=op0),
                reads=reads, writes=writes)

        def dve_stt(out, in0, scalar, in1, op0, op1, reads, writes):
            add("dve", lambda e: e.scalar_tensor_tensor(out=out, in0=in0, scalar=scalar, in1=in1, op0=op0, op1=op1),
                reads=reads, writes=writes)

        ssst = {}

        def rstd_from_ss():
            q = ssst["q"]
            b = psn()
            mm_group(ps[b][:, :], [(onesD[:, :], t16[q][:, :])], reads=[("t16", q), "onesD"], writes=[("ps", b)])
            t = t32n()
            act(t32[t][:, 0:ST], ps[b][:, :], AF.Ln, reads=[("ps", b)], writes=[("t32", t)], bias=EPS)
            act(rstd[:, :], t32[t][:, 0:ST], AF.Exp, reads=[("t32", t)], writes=["rstd"], scale=-0.5)

        def ss_accum(oc, src_ap, src_key):
            if oc == 0:
                ssst["acc"] = t32n()
                a = ssst["acc"]
                ring["hold"] = a
                act(t32[a][:, 0:ST], src_ap, AF.Square, reads=[src_key], writes=[("t32", a)])
                return
            a = ssst["acc"]
            t = t32n()
            act(t32[t][:, 0:ST], src_ap, AF.Square, reads=[src_key], writes=[("t32", t)])
            if oc < KC - 1:
                v_tt("dve", t32[a][:, 0:ST], t32[a][:, 0:ST], t32[t][:, 0:ST], ALU.add,
                     reads=[("t32", a), ("t32", t)], writes=[("t32", a)])
            else:
                q = t16n()
                ssst["q"] = q
                v_tt("dve", t16[q][:, :], t32[a][:, 0:ST], t32[t][:, 0:ST], ALU.add,
                     reads=[("t32", a), ("t32", t)], writes=[("t16", q)])
                ring["hold"] = None

        def pre_norm(kind, l, hbuf, hk):
            for c in range(KC):
                ss_accum(c, xT[:, c, :], ("xT", c))
            rstd_from_ss()
            for c in range(KC):
                dve_stt(hbuf[:, c, :], xT[:, c, :], ppc(gidx(kind, l, c)), rstd[:, :], ALU.mult, ALU.mult,
                        reads=[("xT", c), "rstd", "pp"], writes=[(hk, c)])

        def post_norm_residual(kind, l):
            rstd_from_ss()
            for c in range(KC):
                t = t32n()
                dve_stt(t32[t][:, 0:ST], yT[:, c, :], ppc(gidx(kind, l, c)), rstd[:, :], ALU.mult, ALU.mult,
                        reads=[("yT", c), "rstd", "pp"], writes=[("t32", t)])
                v_tt("pool" if c % 2 == 0 else "dve", xT[:, c, :], xT[:, c, :], t32[t][:, 0:ST], ALU.add,
                     reads=[("xT", c), ("t32", t)], writes=[("xT", c)])

        def conv3(src_ps, src_key, halo, hkey, wi, out_ap, out_key, mul_ap=None, mul_key=None):
            t = t32n()
            ub = t32[t]
            if mul_ap is None:
                v_copy("dve", ub[:, 2:ST + 2], src_ps, reads=[src_key], writes=[("t32", t)])
            else:
                v_tt("dve", ub[:, 2:ST + 2], src_ps, mul_ap, ALU.mult, reads=[src_key, mul_key], writes=[("t32", t)])
            v_copy("dve", ub[:, 0:2], halo, reads=[hkey], writes=[("t32", t)])
            v_copy("dve", halo, ub[:, ST:ST + 2], reads=[("t32", t)], writes=[hkey])
            act(out_ap, ub[:, 0:ST], AF.Copy, reads=[("t32", t), "pp"], writes=[out_key], scale=ppc(wi[0]))
            dve_stt(out_ap, ub[:, 1:ST + 1], ppc(wi[1]), out_ap, ALU.mult, ALU.add,
                    reads=[("t32", t), "pp", out_key], writes=[out_key])
            dve_stt(out_ap, ub[:, 2:ST + 2], ppc(wi[2]), out_ap, ALU.mult, ALU.add,
                    reads=[("t32", t), "pp", out_key], writes=[out_key])

        def proj_fm(wt, wkey, j, rhs_fn, rhs_keys, nkc):
            b = psn()
            mm_group(ps[b][:, :], [(wt[:, kc, j * P:(j + 1) * P], rhs_fn(kc)) for kc in range(nkc)],
                     reads=[wkey] + list(rhs_keys), writes=[("ps", b)])
            return b

        def src_of(l):
            return xT_d if l == 0 else xs_d[(l - 1) % 2]

        def load_x(l, s):
            srcv = src_of(l).rearrange("(c p) t -> p c t", p=P)[:, :, s * ST:(s + 1) * ST]
            add("sp", lambda e: e.dma_start(out=xT[:, :, :], in_=srcv),
                reads=([("xs", (l - 1) % 2, s)] if l > 0 else []), writes=[("xT", c) for c in range(KC)],
                dkey="xld")

        def stage_d1(l, s, p):
            t0, t1 = s * ST, (s + 1) * ST
            hb, hk = hT[p], "hT%d" % p
            hkeys = [(hk, c) for c in range(KC)]
            hfn = lambda kc: hb[:, kc, :]
            load_x(l, s)
            if s == 0:
                add("dve", lambda e: e.memset(halo_m[:, :, :], 0.0), writes=[("hm", c) for c in range(4)])
            pre_norm(0, l, hb, hk)
            wt, wk = wtile("w_in", l, 0, KC, C_Q, 512)
            for j in range(4):
                b = proj_fm(wt, wk, j, hfn, hkeys, KC)
                act(qT[:, j, :], ps[b][:, :], AF.Copy, reads=[("ps", b)], writes=[("qT", j)], scale=-0.125)
            wt, wk = wtile("w_in", l, 0, KC, C_K, 512)
            for j in range(4):
                b = proj_fm(wt, wk, j, hfn, hkeys, KC)
                v_copy("dve", kT[:, j, t0:t1], ps[b][:, :], reads=[("ps", b)], writes=[("kT", j, s)])
            wt, wk = wtile("w_in", l, 0, KC, C_V, 512)
            for tb in range(4):
                b = psn()
                mm_group(ps[b][:, :], [(hb[:, kc, tb * P:(tb + 1) * P], wt[:, kc, :]) for kc in range(KC)],
                         reads=[wk] + hkeys, writes=[("ps", b)])
                if tb % 2 == 0:
                    act(vS[:, 4 * s + tb, :], ps[b][:, :], AF.Copy, reads=[("ps", b)], writes=[("vS", 4 * s + tb)])
                else:
                    v_copy("dve", vS[:, 4 * s + tb, :], ps[b][:, :], reads=[("ps", b)], writes=[("vS", 4 * s + tb)])
            wt, wk = wtile("w_in", l, 0, KC, C_CC, 512)
            for j in range(4):
                b = proj_fm(wt, wk, j, hfn, hkeys, KC)
                act(yT[:, j, :], ps[b][:, :], AF.Copy, reads=[("ps", b)], writes=[("yT", j)])
            wt, wk = wtile("w_in", l, 0, KC, C_CX, 512)
            for j in range(4):
                b = proj_fm(wt, wk, j, hfn, hkeys, KC)
                conv3(ps[b][:, :], ("ps", b), halo_m[:, j, :], ("hm", j),
                      [cmidx(l, i, j) for i in range(3)], yT[:, j, :], ("yT", j),
                      mul_ap=yT[:, j, :], mul_key=("yT", j))
            wt, wk = wtile("w_in", l, 0, KC, C_CB, 512)
            for j in range(4):
                b = proj_fm(wt, wk, j, hfn, hkeys, KC)
                v_tt("dve", cvT[p][:, j, :], ps[b][:, :], yT[:, j, :], ALU.mult,
                     reads=[("ps", b), ("yT", j)], writes=[("cvT%d" % p, j)])

        def stage_attn(l, s, p):
            steps = []
            for h in range(8):
                kbs = list(range(4 * s + 3, -1, -1))
                for i, kb in enumerate(kbs):
                    j = kb - 4 * s
                    steps.append(dict(h=h, kb=kb, c0=(j * P if j >= 0 else 0), first=(i == 0),
                                      last=(i == len(kbs) - 1), diag=(j >= 0)))
            n = len(steps)

            def common(st):
                h = st["h"]
                return h, h // 2, (h % 2) * 64, st["kb"], st["c0"]

            def stageA_pe(i):
                st = steps[i]
                h, c, p0, kb, c0 = common(st)
                zb = ZB[i % NZ]
                mm_group(ps[zb][:, c0:ST], [(kT[p0:p0 + 64, c, kb * P:(kb + 1) * P], qT[p0:p0 + 64, c, c0:ST])],
                         reads=[("kT", c, kb // 4), ("qT", c)], writes=[("ps", zb)], start=True, stop=True)

            def stageA_act(i):
                st = steps[i]
                h, c, p0, kb, c0 = common(st)
                zb = ZB[i % NZ]
                eb = i % 2
                sp = i % NSP
                if st["first"]:
                    add("dve", (lambda r: lambda e: e.memset(aR[r][:, :], 0.0))(h % 2), writes=[("aR", h % 2)])
                act(aE[eb][:, c0:ST], ps[zb][:, c0:ST], AF.Exp, reads=[("ps", zb)], writes=[("aE", eb)], scale=-1.0)
                act(aSP[sp][:, c0:ST], aE[eb][:, c0:ST], AF.Ln, reads=[("aE", eb)], writes=[("aSP", sp)], bias=1.0)
                if st["diag"]:
                    v_tt("dve", aSP[sp][:, c0:c0 + P], aSP[sp][:, c0:c0 + P], msk[:, :], ALU.mult,
                         reads=[("aSP", sp), "msk"], writes=[("aSP", sp)])

            def stageB_pe(i):
                st = steps[i]
                h, c, p0, kb, c0 = common(st)
                tb = TB[i % NZ]
                sp = i % NSP
                r = h % 2
                pairs = [(triI[:, :], aSP[sp][:, c0:ST])]
                reads = ["triI", ("aSP", sp)]
                if not st["first"]:
                    pairs.append((ones1[:, :], aR[r][:, c0:ST]))
                    reads += ["ones1", ("aR", r)]
                mm_group(ps[tb][:, c0:ST], pairs, reads=reads, writes=[("ps", tb)], start=False, stop=True,
                         nocheck=True)
                if not st["last"]:
                    v_tt("dve", aR[r][:, c0:ST], aR[r][:, c0:ST], aSP[sp][:, c0:ST], ALU.add,
                         reads=[("aR", r), ("aSP", sp)], writes=[("aR", r)])

            def stageB_act(i):
                st = steps[i]
                h, c, p0, kb, c0 = common(st)
                tb = TB[i % NZ]
                wb, wk = aW[i % NW], ("aW", i % NW)
                act(wb[:, c0:ST], ps[tb][:, c0:ST], AF.Exp, reads=[("ps", tb)], writes=[wk], scale=-1.0)
                if st["diag"]:
                    v_tt("dve", wb[:, c0:c0 + P], wb[:, c0:c0 + P], msk[:, :], ALU.mult, reads=[wk, "msk"], writes=[wk])

            def stageC(i):
                st = steps[i]
                h, c, p0, kb, c0 = common(st)
                ob = OB[h % 2]
                wb, wk = aW[i % NW], ("aW", i % NW)
                mm_group(ps[ob][p0:p0 + 64, c0:ST], [(vS[:, kb, h * 64:(h + 1) * 64], wb[:, c0:ST])],
                         reads=[("vS", kb), wk], writes=[("ps", ob)], start=st["first"], stop=st["last"],
                         nocheck=True)
                if st["last"]:
                    v_copy("dve", oT[p][p0:p0 + 64, c, :], ps[ob][p0:p0 + 64, :],
                           reads=[("ps", ob)], writes=[("oT%d" % p, c)])

            for t in range(n + 4):
                if t < n:
                    stageA_pe(t)
                if 0 <= t - 1 < n:
                    stageA_act(t - 1)
                if 0 <= t - 2 < n:
                    stageB_pe(t - 2)
                if 0 <= t - 3 < n:
                    stageB_act(t - 3)
                if 0 <= t - 4 < n:
                    stageC(t - 4)

        def sticky():
            cur["gid"] += 1
            cur["grp"] = cur["gid"]

        def unsticky():
            cur["grp"] = None

        def stage_d2(l, s, p):
            t0, t1 = s * ST, (s + 1) * ST
            hb, hk = hT[p], "hT%d" % p
            hkeys = [(hk, c) for c in range(KC)]
            hfn = lambda kc: hb[:, kc, :]
            ok_ = "oT%d" % p
            ck_ = "cvT%d" % p
            ofn = lambda kc: oT[p][:, kc, :]
            okeys = [(ok_, kc) for kc in range(4)]
            cfn = lambda kc: cvT[p][:, kc, :]
            ckeys = [(ck_, kc) for kc in range(4)]
            load_x(l, s)
            if s == 0:
                add("dve", lambda e: e.memset(halo_f[:, :, :], 0.0), writes=[("hf", c) for c in range(44)])

            def gate_sig(b, j):
                act(yT[:, j, :], ps[b][:, :], AF.Exp, reads=[("ps", b)], writes=[("yT", j)], scale=-1.0)
                act(yT[:, j, :], yT[:, j, :], AF.Ln, reads=[("yT", j)], writes=[("yT", j)], bias=1.0)
                act(yT[:, j, :], yT[:, j, :], AF.Exp, reads=[("yT", j)], writes=[("yT", j)], scale=-1.0)

            for half in range(2):
                wt, wk = wtile("w_in", l, 0, KC, C_GA + half * 512, 512)
                for j in range(4):
                    b = proj_fm(wt, wk, j, hfn, hkeys, KC)
                    gate_sig(b, j)
                wt, wk = wtile("w_att_branch", l, 0, 4, half * 512, 512)
                for j in range(4):
                    b = proj_fm(wt, wk, j, ofn, okeys, 4)
                    v_tt("dve", yT[:, j, :], ps[b][:, :], yT[:, j, :], ALU.mult,
                         reads=[("ps", b), ("yT", j)], writes=[("yT", j)])
                wt, wk = wtile("w_in", l, 0, KC, C_GC + half * 512, 512)
                for j in range(4):
                    b = proj_fm(wt, wk, j, hfn, hkeys, KC)
                    gate_sig(b, 4 + j)
                wt, wk = wtile("w_conv_branch", l, 0, 4, half * 512, 512)
                for j in range(4):
                    b = proj_fm(wt, wk, j, cfn, ckeys, 4)
                    v_tt("dve", yT[:, 4 + j, :], ps[b][:, :], yT[:, 4 + j, :], ALU.mult,
                         reads=[("ps", b), ("yT", 4 + j)], writes=[("yT", 4 + j)])
                    oc = half * 4 + j
                    v_tt("pool", fT[:, 12 + oc, :], yT[:, j, :], yT[:, 4 + j, :], ALU.add,
                         reads=[("yT", j), ("yT", 4 + j)], writes=[("fT", 12 + oc)])
            mfn = lambda kc: fT[:, 12 + kc, :]
            mkeys = [("fT", 12 + kc) for kc in range(KC)]
            for half in range(2):
                wt, wk = wtile("w_out", l, 0, KC, half * 512, 512)
                for j in range(4):
                    oc = half * 4 + j
                    b = proj_fm(wt, wk, j, mfn, mkeys, KC)
                    v_copy("dve", yT[:, oc, :], ps[b][:, :], reads=[("ps", b)], writes=[("yT", oc)])
                    ss_accum(oc, yT[:, oc, :], ("yT", oc))
            post_norm_residual(1, l)
            pre_norm(2, l, hb, hk)
            for grp in range(6):
                nj = 4 if grp < 5 else 2
                wt, wk = wtile("w_up", l, 0, KC, grp * 512, nj * P)
                for j in range(nj):
                    ch = grp * 4 + j
                    b = proj_fm(wt, wk, j, hfn, hkeys, KC)
                    conv3(ps[b][:, :], ("ps", b), halo_f[:, ch, :], ("hf", ch),
                          [cfidx(l, i, ch) for i in range(3)], yT[:, j, :], ("yT", j))
                wt, wk = wtile("w_up", l, 0, KC, DFF + grp * 512, nj * P)
                for j in range(nj):
                    ch = FC + grp * 4 + j
                    b = proj_fm(wt, wk, j, hfn, hkeys, KC)
                    conv3(ps[b][:, :], ("ps", b), halo_f[:, ch, :], ("hf", ch),
                          [cfidx(l, i, ch) for i in range(3)], yT[:, 4 + j, :], ("yT", 4 + j))
                sticky()
                for j in range(nj):
                    act(yT[:, 4 + j, :], yT[:, 4 + j, :], AF.Gelu_apprx_tanh,
                        reads=[("yT", 4 + j)], writes=[("yT", 4 + j)])
                unsticky()
                for j in range(nj):
                    v_tt("pool" if j % 2 == 0 else "dve", fT[:, grp * 4 + j, :], yT[:, 4 + j, :], yT[:, j, :], ALU.mult,
                         reads=[("yT", j), ("yT", 4 + j)], writes=[("fT", grp * 4 + j)])
            for cg in range(4):
                banks = [psn(), psn()]
                for kh in range(2):
                    wt, wk = wtile("w_down", l, kh * 11, 11, cg * 256, 256)
                    for j in range(2):
                        b = banks[j]
                        mm_group(ps[b][:, :],
                                 [(wt[:, kk, j * P:(j + 1) * P], fT[:, kh * 11 + kk, :]) for kk in range(11)],
                                 reads=[wk] + [("fT", kh * 11 + kk) for kk in range(11)], writes=[("ps", b)],
                                 start=(kh == 0), stop=(kh == 1))
                for j in range(2):
                    oc = cg * 2 + j
                    b = banks[j]
                    v_copy("dve", yT[:, oc, :], ps[b][:, :], reads=[("ps", b)], writes=[("yT", oc)])
                    ss_accum(oc, yT[:, oc, :], ("yT", oc))
            post_norm_residual(3, l)
            last = (l == NL - 1)
            dst = out_d if last else xs_d[l % 2]
            dstv = dst.rearrange("(c p) t -> p c t", p=P)[:, :, t0:t1]
            add("sp", lambda e: e.dma_start(out=dstv, in_=xT[:, :, :]),
                reads=[("xT", c) for c in range(KC)],
                writes=[("out", s) if last else ("xs", l % 2, s)], dkey="xst")

        def gen(fn, *a):
            saved = sch.ops
            sch.ops = []
            fn(*a)
            out = sch.ops
            sch.ops = saved
            return out

        class _FakePE:
            def __init__(self):
                self.n = 0

            def matmul(self, *a, **k):
                self.n += 1
                return self

        def op_cost(op):
            if op.fn is None:
                return 0.0
            if op.eng == "pe":
                f = _FakePE()
                op.fn(f)
                return 0.25 * f.n
            if op.eng == "act":
                return 0.62
            if op.eng == "dve":
                return 0.6
            if op.eng == "pool":
                return 1.45
            return 0.1

        def stream_deps(ops):
            last_w, readers, deps = {}, {}, []
            for i, op in enumerate(ops):
                d = set()
                for k in op.reads:
                    if k in last_w:
                        d.add(last_w[k])
                for k in op.writes:
                    if k in last_w:
                        d.add(last_w[k])
                    d.update(readers.get(k, ()))
                d.discard(i)
                deps.append(d)
                for k in op.reads:
                    readers.setdefault(k, []).append(i)
                for k in op.writes:
                    last_w[k] = i
                    readers[k] = []
            return deps

        def merge_ls(A, B):
            streams = [A, B]
            deps = [stream_deps(A), stream_deps(B)]
            fin = [[0.0] * len(A), [0.0] * len(B)]
            idx = [0, 0]
            free = {"pe": 0.0, "act": 0.0, "dve": 0.0, "pool": 0.0, "sp": 0.0}
            out = []

            def est(k):
                i = idx[k]
                op = streams[k][i]
                ready = max([fin[k][j] for j in deps[k][i]], default=0.0)
                return max(ready, free[op.eng])

            def emit_one(k):
                i = idx[k]
                op = streams[k][i]
                st = est(k)
                c = op_cost(op)
                if op.dkey is not None:
                    free[op.eng] = st + 0.1
                    fin[k][i] = st + 4.0
                else:
                    free[op.eng] = st + c
                    fin[k][i] = st + c
                out.append(op)
                idx[k] += 1

            while idx[0] < len(A) or idx[1] < len(B):
                if idx[0] >= len(A):
                    k = 1
                elif idx[1] >= len(B):
                    k = 0
                else:
                    k = 0 if est(0) <= est(1) else 1
                g = streams[k][idx[k]].name
                emit_one(k)
                if g is not None and not isinstance(g, tuple):
                    while idx[k] < len(streams[k]) and streams[k][idx[k]].name == g:
                        emit_one(k)
            return out

        def merge(A, B):
            out = []
            ia = ib = 0
            na, nb = len(A), len(B)
            while ia < na or ib < nb:
                take_b = (ia >= na) or (ib < nb and ib * na <= ia * nb)
                if take_b:
                    g = B[ib].name
                    out.append(B[ib])
                    ib += 1
                    if g is not None and not isinstance(g, tuple):
                        while ib < nb and B[ib].name == g:
                            out.append(B[ib])
                            ib += 1
                else:
                    out.append(A[ia])
                    ia += 1
            return out

        for op in conv_ops(0):
            sch.ops.append(op)
        base_mark = sch.add("pe", None)
        base_mark.name = ("wmark", -1)
        G = NL * NS
        pend_conv = []
        for g in range(G + 1):
            l, s = divmod(g, NS)
            if g < G and s == 0 and l + 1 < NL:
                pend_conv = conv_ops(l + 1)
            if g < G:
                per = (len(WNAMES) and (sum(WPIECES.values()) + NS - 1) // NS)
                for op in pend_conv[:per]:
                    sch.ops.append(op)
                pend_conv = pend_conv[per:]
                ring["set"] = RING_D1
                sch.ops.extend(gen(stage_d1, l, s, g % 2))
                ring["set"] = RING_D2
            A = gen(stage_attn, l, s, g % 2) if g < G else []
            B = []
            if g >= 1:
                lp, sp_ = divmod(g - 1, NS)
                B = gen(stage_d2, lp, sp_, (g - 1) % 2)
            if INTERLEAVE:
                sch.ops.extend(merge(A, B))
            else:
                sch.ops.extend(A)
                sch.ops.extend(B)
        sch.add("sp", None, reads=[("out", s) for s in range(NS)])

        before = {}
        for i, op in enumerate(wstate["loads"]):
            before.setdefault(max(i - (NSLOT - 1), -1), []).append(op)
        merged = []
        for op in sch.ops:
            if isinstance(op.name, tuple) and op.name[0] == "wmark":
                merged.extend(before.get(op.name[1], []))
                continue
            merged.append(op)
        sch.ops = merged
        sch.analyze()
        emit(nc, sch, es)
    return nc


def emit(nc, sch, es):
    engs = ("pe", "act", "dve", "pool", "sp")
    esem = {e: es.enter_context(nc.semaphore("sem_" + e)) for e in engs}
    dsem = {}
    for op in sch.ops:
        if op.dkey is not None and op.dkey not in dsem:
            dsem[op.dkey] = es.enter_context(nc.semaphore("d_" + op.dkey))
    cnt = {e: 0 for e in engs}
    dcnt = {k: 0 for k in dsem}
    for op in sch.ops:
        if op.dkey is not None:
            dcnt[op.dkey] += 16
            op.tok = (dsem[op.dkey], dcnt[op.dkey], op.dkey)
        elif op.signal:
            cnt[op.eng] += 1
            op.tok = (esem[op.eng], cnt[op.eng], op.eng)
    by_eng = {e: [] for e in engs}
    for op in sch.ops:
        by_eng[op.eng].append(op)

    def run(eng_name, e):
        known = {}
        for op in by_eng[eng_name]:
            need = {}
            for d in op.deps:
                sem, val, key = d.tok
                if known.get(key, 0) >= val:
                    continue
                if need.get(key, (None, 0))[1] < val:
                    need[key] = (sem, val)
            for key, (sem, val) in need.items():
                e.wait_ge(sem, val)
                known[key] = val
            if op.fn is None:
                continue
            ins = op.fn(e)
            if op.dkey is not None:
                ins.then_inc(op.tok[0], 16)
            elif op.signal:
                ins.then_inc(op.tok[0], 1)

    with nc.Block() as block:
        @block.tensor
        def _(e):
            run("pe", e)

        @block.scalar
        def _(e):
            run("act", e)

        @block.vector
        def _(e):
            run("dve", e)

        @block.gpsimd
        def _(e):
            run("pool", e)

        @block.sync
        def _(e):
            run("sp", e)


_CACHE = {}


def run_model(inputs, NL, NB, NS):
    key = (NL, NS)
    if key not in _CACHE:
        _CACHE[key] = build_program(NL, NS)
    nc = _CACHE[key]
    x = np.asarray(inputs["x"], np.float32)
    pp = pack_params(inputs, NL)
    cm = const_mats()
    shared = {"pp": pp, "cmat": cm}
    for nm in WNAMES:
        shared[nm] = np.ascontiguousarray(np.asarray(inputs[nm], np.float32)[:NL])
    in_maps = []
    for b in range(NB):
        m = dict(shared)
        m["xT"] = np.ascontiguousarray(x[b].T)
        in_maps.append(m)
    res = run_bass_kernel_spmd(nc, in_maps, core_ids=list(range(NB)))
    out = np.stack([np.asarray(r["outT"], np.float32).T for r in res.results], axis=0)
    return np.ascontiguousarray(out)


def kernel(**inputs):
    return run_model(inputs, DEPTH, 8, SEQ // ST)
```
